# Optimizing a Trainium2 kernel written in Bass

```python
import math
import jax, jax.numpy as jnp
from jax import lax
import numpy as np

D_MODEL = 1024
BATCH = 16
SEQ = 2048
DEPTH = 4

N_META = 16
CHUNK = 64
N_PAD = (-N_META) % CHUNK
DN_HEADS = 4
DN_HEAD_DIM = 128
DN_WIDTH = DN_HEADS * DN_HEAD_DIM
SC_GROUPS = 4
SC_WIDTH = D_MODEL - DN_WIDTH
MIX_WIDTH = DN_WIDTH + SC_WIDTH
DN_CONV = 3
SC_CONV = 3
D_FF = ((-(-8 * D_MODEL // 3) + 255) // 256) * 256
RMS_EPS = 1e-6
L2_EPS = 1e-6
IN_SIZES = (3 * DN_WIDTH, DN_WIDTH, 2 * DN_HEADS, 2 * DN_HEADS, SC_WIDTH, SC_WIDTH, SC_WIDTH)
IN_WIDTH = 4 * DN_WIDTH + 4 * DN_HEADS + 3 * SC_WIDTH

kernel_name = "hymba_gdn_shortconv_encoder"


def rmsnorm(x, w):
    xf = x.astype(jnp.float32)
    y = xf * lax.rsqrt(jnp.mean(xf * xf, axis=-1, keepdims=True) + RMS_EPS)
    return (y * w.astype(jnp.float32)).astype(x.dtype)


def l2norm(x):
    xf = x.astype(jnp.float32)
    return xf * lax.rsqrt(jnp.sum(xf * xf, axis=-1, keepdims=True) + L2_EPS)


def depthwise_conv_centred(x, w):
    K, C = w.shape
    return lax.conv_general_dilated(
        x, w.astype(x.dtype)[:, None, :], window_strides=(1,),
        padding=[(K // 2, K // 2)], dimension_numbers=('NWC', 'WIO', 'NWC'),
        feature_group_count=C)


def chunk_gated_delta_rule(q, k, v, g, beta):
    Bsz, T, H, Dk = q.shape
    Dv = v.shape[-1]
    N = T // CHUNK

    def to_chunks(t):
        return jnp.swapaxes(t.reshape((Bsz, N, CHUNK) + t.shape[2:]), 2, 3)

    q = to_chunks(q) * (Dk ** -0.5)
    k = to_chunks(k)
    v = to_chunks(v)
    g = to_chunks(g)
    beta = to_chunks(beta)

    G = jnp.cumsum(g, axis=-1)
    incl = jnp.tril(jnp.ones((CHUNK, CHUNK), dtype=bool))
    strict = jnp.tril(jnp.ones((CHUNK, CHUNK), dtype=bool), -1)
    decay_mat = jnp.exp(jnp.where(incl, G[..., :, None] - G[..., None, :], -jnp.inf))

    kb = k * beta[..., None]
    vb = v * beta[..., None]
    M = jnp.where(strict, jnp.einsum('bnhid,bnhjd->bnhij', kb, k) * decay_mat, 0.0)
    A = M + jnp.eye(CHUNK, dtype=M.dtype)
    u = lax.linalg.triangular_solve(A, vb, left_side=True, lower=True, unit_diagonal=True)
    w = lax.linalg.triangular_solve(A, kb * jnp.exp(G)[..., None], left_side=True, lower=True,
                                    unit_diagonal=True)
    attn = jnp.einsum('bnhid,bnhjd->bnhij', q, k) * decay_mat
    q_dec = q * jnp.exp(G)[..., None]
    k_dec = k * jnp.exp(G[..., -1:] - G)[..., None]
    g_last = jnp.exp(G[..., -1])

    def step(S, inp):
        u_c, w_c, qd_c, kd_c, a_c, gl_c = inp
        v_new = u_c - jnp.einsum('bhcd,bhde->bhce', w_c, S)
        o = jnp.einsum('bhcd,bhde->bhce', qd_c, S) + jnp.einsum('bhij,bhje->bhie', a_c, v_new)
        S = S * gl_c[..., None, None] + jnp.einsum('bhcd,bhce->bhde', kd_c, v_new)
        return S, o

    S0 = jnp.zeros((Bsz, H, Dk, Dv), jnp.float32)
    xs = tuple(jnp.moveaxis(t, 1, 0) for t in (u, w, q_dec, k_dec, attn, g_last))
    _, o = lax.scan(step, S0, xs)
    o = jnp.transpose(o, (1, 0, 3, 2, 4))
    return o.reshape(Bsz, T, H, Dv)


def hybrid_mixer(h, w_in, conv_qkv_w, a_log, dt_bias, dn_norm_w, sc_conv_w, w_out):
    Bsz, L, _ = h.shape
    proj = h @ w_in.astype(h.dtype)
    idx = list(np.cumsum(IN_SIZES)[:-1])
    qkv, z, b_raw, a_raw, sc_b, sc_c, sc_h = jnp.split(proj, idx, axis=-1)

    qkv = jax.nn.silu(depthwise_conv_centred(qkv, conv_qkv_w))
    q, k, v = jnp.split(qkv, 3, axis=-1)
    q = l2norm(q.reshape(Bsz, L, DN_HEADS, DN_HEAD_DIM))
    k = l2norm(k.reshape(Bsz, L, DN_HEADS, DN_HEAD_DIM))
    v = v.reshape(Bsz, L, DN_HEADS, DN_HEAD_DIM).astype(jnp.float32)
    beta = jax.nn.sigmoid(b_raw.astype(jnp.float32)).reshape(Bsz, L, 2, DN_HEADS)
    g = -jnp.exp(a_log.astype(jnp.float32)) * jax.nn.softplus(
        a_raw.astype(jnp.float32).reshape(Bsz, L, 2, DN_HEADS) + dt_bias.astype(jnp.float32))

    def lpad(t):
        return jnp.pad(t, ((0, 0), (N_PAD, 0)) + ((0, 0),) * (t.ndim - 2))

    def flip(t):
        return jnp.flip(t, axis=1)

    qp, kp, vp, gp, bp = lpad(q), lpad(k), lpad(v), lpad(g), lpad(beta)
    o_fwd = chunk_gated_delta_rule(qp, kp, vp, gp[:, :, 0], bp[:, :, 0])
    o_bwd = flip(chunk_gated_delta_rule(flip(qp), flip(kp), flip(vp),
                                        flip(gp[:, :, 1]), flip(bp[:, :, 1])))
    o = (o_fwd + o_bwd)[:, N_PAD:]
    o = o * lax.rsqrt(jnp.mean(o * o, axis=-1, keepdims=True) + RMS_EPS)
    zf = z.astype(jnp.float32).reshape(Bsz, L, DN_HEADS, DN_HEAD_DIM)
    o = (o * dn_norm_w.astype(jnp.float32) * jax.nn.silu(zf)).reshape(Bsz, L, DN_WIDTH)
    o = o.astype(h.dtype)

    y_sc = sc_b * depthwise_conv_centred(sc_c * sc_h, sc_conv_w)

    mix = jnp.concatenate([o, y_sc], axis=-1)
    return mix @ w_out.astype(h.dtype)


def swiglu(h, w_gate_up, w_down):
    gate, up = jnp.split(h @ w_gate_up.astype(h.dtype), 2, axis=-1)
    return (jax.nn.silu(gate) * up) @ w_down.astype(h.dtype)


def setup_inputs(seed: int = 0) -> dict:
    key = jax.random.key(seed)
    ks = jax.random.split(key, 16)
    f32 = jnp.float32
    x = jax.random.normal(ks[0], (BATCH, SEQ, D_MODEL), f32)
    meta_tokens = jax.random.normal(ks[1], (N_META, D_MODEL), f32)
    norm1_w = 1.0 + 0.01 * jax.random.normal(ks[2], (DEPTH, D_MODEL), f32)
    w_in = jax.random.normal(ks[3], (DEPTH, D_MODEL, IN_WIDTH), f32) * D_MODEL ** -0.5
    conv_qkv_w = jax.random.normal(ks[4], (DEPTH, DN_CONV, 3 * DN_WIDTH), f32) * DN_CONV ** -0.5
    a_log = jnp.log(jax.random.uniform(ks[5], (DEPTH, 2, DN_HEADS), f32, minval=1.0, maxval=16.0))
    dt = jnp.exp(jax.random.uniform(ks[6], (DEPTH, 2, DN_HEADS), f32,
                                    minval=math.log(1e-3), maxval=math.log(1e-1)))
    dt_bias = dt + jnp.log(-jnp.expm1(-dt))
    dn_norm_w = 1.0 + 0.01 * jax.random.normal(ks[7], (DEPTH, DN_HEAD_DIM), f32)
    sc_conv_w = jax.random.normal(ks[8], (DEPTH, SC_CONV, SC_WIDTH), f32) * SC_CONV ** -0.5
    w_out = jax.random.normal(ks[9], (DEPTH, MIX_WIDTH, D_MODEL), f32) * MIX_WIDTH ** -0.5
    norm2_w = 1.0 + 0.01 * jax.random.normal(ks[10], (DEPTH, D_MODEL), f32)
    w_gate_up = jax.random.normal(ks[11], (DEPTH, D_MODEL, 2 * D_FF), f32) * D_MODEL ** -0.5
    w_down = jax.random.normal(ks[12], (DEPTH, D_FF, D_MODEL), f32) * D_FF ** -0.5
    final_norm_w = 1.0 + 0.01 * jax.random.normal(ks[13], (D_MODEL,), f32)
    return {"x": x, "meta_tokens": meta_tokens, "norm1_w": norm1_w, "w_in": w_in,
            "conv_qkv_w": conv_qkv_w, "a_log": a_log, "dt_bias": dt_bias,
            "dn_norm_w": dn_norm_w, "sc_conv_w": sc_conv_w, "w_out": w_out,
            "norm2_w": norm2_w, "w_gate_up": w_gate_up, "w_down": w_down,
            "final_norm_w": final_norm_w}


def reference(x, meta_tokens, norm1_w, w_in, conv_qkv_w, a_log, dt_bias, dn_norm_w,
              sc_conv_w, w_out, norm2_w, w_gate_up, w_down, final_norm_w):
    Bsz = x.shape[0]
    meta = jnp.broadcast_to(meta_tokens.astype(x.dtype)[None], (Bsz, N_META, D_MODEL))
    h = jnp.concatenate([meta, x], axis=1)
    for l in range(DEPTH):
        h = h + hybrid_mixer(rmsnorm(h, norm1_w[l]), w_in[l], conv_qkv_w[l], a_log[l],
                             dt_bias[l], dn_norm_w[l], sc_conv_w[l], w_out[l])
        h = h + swiglu(rmsnorm(h, norm2_w[l]), w_gate_up[l], w_down[l])
    h = rmsnorm(h, final_norm_w)
    return h[:, N_META:]
```

```python
import numpy as np
from contextlib import ExitStack
import concourse.bass as bass
import concourse.mybir as mybir
from concourse.bass_utils import run_bass_kernel_spmd

F32 = mybir.dt.float32
BF16 = mybir.dt.bfloat16
ALU = mybir.AluOpType
AF = mybir.ActivationFunctionType
AX = mybir.AxisListType

ENGS = ("pe", "act", "dve", "pool", "sp")
N_DMA_SEMS = 16
N_DMA_SEMS_SW = 8
SEM_ROLL = 4000
SAME_ENG_SYNC = True

D = 1024
KD = 8
L_FULL = 4
BATCH = 16
SEQ = 2048
NMETA = 16
NCORES = 8
SPC = 2
NCH = 17
ST = NCH * 128
NPAD = ST - SEQ - NMETA
TOK = SPC * ST
H = 4
DFF = 2816
NF = DFF // 128
INW = 3600
FGROUPS = [(0, 6), (6, 12), (12, 17), (17, 22)]
NEGBIG = -30000.0

O_NW1, O_NW2, O_FNW, O_CQ, O_CS, O_DN, O_AL, O_DT = 0, 32, 64, 72, 216, 264, 268, 300
NPRM = 332


class Sched:
    def __init__(self):
        self.ops = []
        self.eng_ops = {e: [] for e in ENGS}
        self.last_w = {}
        self.readers = {}
        self.dma_count = {"hw": 0, "sw": 0}
        self.dma_slot_last = {"hw": [None] * N_DMA_SEMS, "sw": [None] * N_DMA_SEMS_SW}
        self.bar_deps = {e: set() for e in ENGS}

    def add(self, eng, fn, reads=(), writes=(), dma=False):
        deps = set()
        for k in reads:
            w = self.last_w.get(k)
            if w is not None:
                deps.add(w)
            if isinstance(k, tuple) and k[0] == "ps":
                for r_ in self.readers.get(k, ()):
                    if self.ops[r_]["eng"] != eng:
                        deps.add(r_)
        for k in writes:
            w = self.last_w.get(k)
            if w is not None:
                deps.add(w)
            deps.update(self.readers.get(k, ()))
        if self.bar_deps[eng]:
            deps |= self.bar_deps[eng]
            self.bar_deps[eng] = set()
        newest = {}
        pruned = set()
        for d_ in deps:
            od = self.ops[d_]
            if od["dma"]:
                pruned.add(d_)
            else:
                e_ = od["eng"]
                if e_ not in newest or newest[e_] < d_:
                    newest[e_] = d_
        deps = pruned | set(newest.values())
        oid = len(self.ops)
        slot = use = None
        if dma:
            qk = "sw" if eng == "pool" else "hw"
            nslots = len(self.dma_slot_last[qk])
            cnt = self.dma_count[qk]
            slot = (qk, cnt % nslots)
            use = cnt // nslots
            self.dma_count[qk] = cnt + 1
            prev = self.dma_slot_last[qk][slot[1]]
            if prev is not None:
                deps.add(prev)
            self.dma_slot_last[qk][slot[1]] = oid
        self.ops.append(dict(eng=eng, fn=fn, deps=deps, dma=dma, slot=slot, use=use,
                             signal=False, sig=None))
        for k in reads:
            self.readers.setdefault(k, []).append(oid)
        for k in writes:
            self.last_w[k] = oid
            self.readers[k] = []
        self.eng_ops[eng].append(oid)
        return oid

    def barrier(self):
        last = set()
        for e in ENGS:
            if self.eng_ops[e]:
                last.add(self.eng_ops[e][-1])
        for qk in ("hw", "sw"):
            for s in self.dma_slot_last[qk]:
                if s is not None:
                    last.add(s)
        for e in ENGS:
            self.bar_deps[e] |= last
        self.last_w = {}
        self.readers = {}

    def _needs_wait(self, dop, op):
        if dop["dma"]:
            return True
        if dop["eng"] == op["eng"]:
            if dop["eng"] == "pe":
                return False
            return SAME_ENG_SYNC
        return True

    def finalize(self, sem_alloc):
        for op in self.ops:
            for d in op["deps"]:
                if self._needs_wait(self.ops[d], op):
                    self.ops[d]["signal"] = True
        for op in self.ops:
            if op["dma"]:
                op["signal"] = True
        for e in ENGS:
            cnt = 0
            sems = []
            for oid in self.eng_ops[e]:
                op = self.ops[oid]
                if op["dma"]:
                    continue
                if op["signal"]:
                    si = cnt // SEM_ROLL
                    while len(sems) <= si:
                        sems.append(sem_alloc("s_%s_%d" % (e, len(sems))))
                    op["sig"] = (sems[si], cnt % SEM_ROLL + 1)
                    cnt += 1
        self.dma_sems = {"hw": [sem_alloc("s_dmah_%d" % i) for i in range(N_DMA_SEMS)],
                         "sw": [sem_alloc("s_dmas_%d" % i) for i in range(N_DMA_SEMS_SW)]}
        for op in self.ops:
            if op["dma"]:
                op["sig"] = (self.dma_sems[op["slot"][0]][op["slot"][1]], 16 * (op["use"] + 1))

    def replay(self, ename, eng, final_wait_all_dma=False):
        waited = {}
        for oid in self.eng_ops[ename]:
            op = self.ops[oid]
            for d in sorted(op["deps"]):
                dop = self.ops[d]
                if not self._needs_wait(dop, op):
                    continue
                sem, val = dop["sig"]
                key = id(sem)
                if waited.get(key, 0) < val:
                    eng.wait_ge(sem, val)
                    waited[key] = val
            ins = op["fn"](eng)
            if op["signal"]:
                ins.then_inc(op["sig"][0], 16 if op["dma"] else 1)
        if final_wait_all_dma:
            for qk in ("hw", "sw"):
                for s in self.dma_slot_last[qk]:
                    if s is not None:
                        sem, val = self.ops[s]["sig"]
                        eng.wait_ge(sem, val)


def emit_program(nc, sched, stack):
    def sem_alloc(name):
        return stack.enter_context(nc.semaphore(name))
    sched.finalize(sem_alloc)
    block = stack.enter_context(nc.Block())

    @block.tensor
    def _(e):
        sched.replay("pe", e)

    @block.scalar
    def _(e):
        sched.replay("act", e)

    @block.vector
    def _(e):
        sched.replay("dve", e)

    @block.gpsimd
    def _(e):
        sched.replay("pool", e)

    @block.sync
    def _(e):
        sched.replay("sp", e, final_wait_all_dma=True)


def tok_blocks(t0, t1, bs=512):
    out = []
    t = t0
    while t < t1:
        n = min(bs, t1 - t)
        out.append((t, n))
        t += n
    return out


class Builder:
    def __init__(self, n_layers, dbg=False, stop=999):
        self.L = n_layers
        self.dbg = dbg
        self.stop = stop
        self.nc = bass.Bass("TRN2", target_bir_lowering=False)
        self.S = Sched()
        self.ps_i = 0

    def mm(self, out, lhsT, rhs, start=True, stop=True, r=(), w=()):
        self.S.add("pe", lambda e: e.matmul(out, lhsT=lhsT, rhs=rhs, start=start, stop=stop), reads=r, writes=w)

    def act(self, out, in_, func, r=(), w=(), **kw):
        self.S.add("act", lambda e: e.activation(out=out, in_=in_, func=func, **kw), reads=r, writes=w)

    def tt(self, eng, out, in0, in1, op, r=(), w=()):
        self.S.add(eng, lambda e: e.tensor_tensor(out=out, in0=in0, in1=in1, op=op), reads=r, writes=w)

    def ts(self, eng, out, in0, s1, s2, op0, op1=ALU.bypass, r=(), w=()):
        self.S.add(eng, lambda e: e.tensor_scalar(out=out, in0=in0, scalar1=s1, scalar2=s2, op0=op0, op1=op1),
                   reads=r, writes=w)

    def stt(self, out, in0, scalar, in1, op0, op1, r=(), w=()):
        self.S.add("dve", lambda e: e.scalar_tensor_tensor(out=out, in0=in0, scalar=scalar, in1=in1, op0=op0, op1=op1),
                   reads=r, writes=w)

    def copy(self, eng, out, in_, r=(), w=()):
        if eng == "act":
            self.S.add("act", lambda e: e.activation(out=out, in_=in_, func=AF.Copy), reads=r, writes=w)
        else:
            self.S.add(eng, lambda e: e.tensor_copy(out=out, in_=in_), reads=r, writes=w)

    def memset(self, eng, ap, val, w=()):
        self.S.add(eng, lambda e: e.memset(ap, val), writes=w)

    def dma(self, q, out, in_, r=(), w=()):
        self.S.add(q, lambda e: e.dma_start(out=out, in_=in_), reads=r, writes=w, dma=True)

    def recip(self, out, in_, r=(), w=()):
        self.S.add("dve", lambda e: e.reciprocal(out=out, in_=in_), reads=r, writes=w)

    def reduce(self, out, in_, r=(), w=()):
        self.S.add("dve", lambda e: e.tensor_reduce(out=out, in_=in_, axis=AX.X, op=ALU.add), reads=r, writes=w)

    def ps(self):
        i = self.ps_i
        self.ps_i = (i + 1) % 7
        return ("ps", i), self.psb[i]

    def view(self, off_w, nwords, dt=F32, pat=None, **kw):
        ap = self.big[:, off_w:off_w + nwords]
        if dt == BF16:
            ap = ap.bitcast(BF16)
        if pat:
            ap = ap.rearrange(pat, **kw)
        return ap

    def build(self):
        nc = self.nc
        L = self.L
        self.x_h0 = nc.dram_tensor("h0", [128, KD, TOK], F32, kind="ExternalInput").ap()
        self.x_prm = nc.dram_tensor("prm", [128, NPRM], F32, kind="ExternalInput").ap()
        self.x_cst = nc.dram_tensor("cst", [128, 9, 128], F32, kind="ExternalInput").ap()
        self.x_win = nc.dram_tensor("w_in", [L_FULL, D, INW], F32, kind="ExternalInput").ap()
        self.x_wout = nc.dram_tensor("w_out", [L_FULL, D, D], F32, kind="ExternalInput").ap()
        self.x_wgu = nc.dram_tensor("w_gu", [L_FULL, D, 2 * DFF], F32, kind="ExternalInput").ap()
        self.x_wd = nc.dram_tensor("w_down", [L_FULL, DFF, D], F32, kind="ExternalInput").ap()
        self.y_out = nc.dram_tensor("outT", [128, KD, TOK], F32, kind="ExternalOutput").ap()
        self.hres = nc.dram_tensor("hres", [128, KD, TOK], F32).ap()
        self.hn2 = nc.dram_tensor("hn2", [128, KD, TOK], BF16).ap()
        self.gsc = nc.dram_tensor("gsc", [128, 8, ST], BF16).ap()
        if self.dbg:
            self.d_qkv = nc.dram_tensor("d_qkv", [128, 12, ST], BF16, kind="ExternalOutput").ap()
            self.d_gb = nc.dram_tensor("d_gb", [128, NCH, 16], F32, kind="ExternalOutput").ap()
            self.d_o = nc.dram_tensor("d_o", [128, NCH, 512], F32, kind="ExternalOutput").ap()
            self.d_mix = nc.dram_tensor("d_mix", [128, 8, ST], BF16, kind="ExternalOutput").ap()
            self.d_h1 = nc.dram_tensor("d_h1", [128, KD, TOK], F32, kind="ExternalOutput").ap()

        with ExitStack() as st:
            def sb(name, shape, dt):
                return st.enter_context(nc.sbuf_tensor(name, shape, dt))
            self.prm = sb("prm_sb", [128, NPRM], F32)
            self.cst = sb("cst_sb", [128, 9, 128], F32)
            self.cbf = sb("cbf_sb", [128, 6, 128], BF16)
            self.ba_tok = sb("ba_tok", [128, NCH, 16], F32)
            self.gb_tok = sb("gb_tok", [128, NCH, 16], F32)
            self.sp_t = sb("sp_t", [128, 4, NCH, 8], F32)
            self.nega = sb("nega", [128, 8], F32)
            self.a1 = sb("a1", [128, 8704], F32)
            self.big = sb("big", [128, 38656], F32)
            self.psb = [st.enter_context(nc.psum_tensor("psb%d" % i, [128, 512], F32)) for i in range(8)]
            self.hn = self.a1[:].bitcast(BF16).rearrange("p (k t) -> p k t", k=KD)
            self.o_acc = self.a1[:].rearrange("p (n c) -> p n c", n=NCH)
            self.qkv = self.view(0, 13056, BF16, "p (c t) -> p c t", c=12)
            self.B0 = 13056

            self.ident_f = self.cst[:, 0, :]
            self.ones_f = self.cst[:, 1, :]
            self.ident_b = self.cbf[:, 0, :]
            self.ones_b = self.cbf[:, 1, :]
            self.mean_b = self.cbf[:, 2, :]

            self.dma("sp", self.prm[:], self.x_prm, w=["prm"])
            self.dma("sp", self.cst[:], self.x_cst, w=["cst"])
            for k in range(KD):
                self.dma("sp", self.hres[:, k, :], self.x_h0[:, k, :], w=[("h0c", k)])
            self.copy("dve", self.cbf[:, 0:2, :], self.cst[:, 0:2, :], r=["cst"], w=["cbf"])
            self.ts("dve", self.cbf[:, 2, :], self.cst[:, 1, :], 1.0 / D, None, ALU.mult, r=["cst", "cbf"], w=["cbf"])
            self.copy("dve", self.cbf[:, 3:6, :], self.cst[:, 6:9, :], r=["cst", "cbf"], w=["cbf"])
            self.S.barrier()

            stage = 0
            for l in range(L):
                for s in range(SPC):
                    for ph in (self.phase_norm1, self.phase_proj, self.phase_delta, self.phase_outproj):
                        if stage >= self.stop:
                            continue
                        stage += 1
                        ph(l, s)
                        self.S.barrier()
                        if self.dbg and l == 0 and s == 0 and ph == self.phase_proj:
                            self.dma("sp", self.d_qkv, self.qkv, w=["d_qkv"])
                            self.dma("sp", self.d_gb, self.gb_tok[:], w=["d_gb"])
                            self.S.barrier()
                        if self.dbg and l == 0 and s == 0 and ph == self.phase_delta:
                            self.dma("sp", self.d_mix, self.qkv[:, 0:8, :], w=["d_mix"])
                            self.S.barrier()
                if stage >= self.stop:
                    continue
                stage += 1
                self.phase_ffn(l)
                self.S.barrier()
                if self.dbg and l == 0:
                    self.dma("sp", self.d_h1, self.hres, w=["d_h1"])
                    self.S.barrier()
            self.phase_final()
            emit_program(nc, self.S, st)
        return nc

    def norm_block(self, hblk, n, sq, rstd, sd, wcol0, out_fn, tag):
        self.act(sq[:, :, 0:n], hblk[:, :, 0:n], AF.Square, r=[tag + "h"], w=[tag + "sq"])
        pk, pt = self.ps()
        for k in range(KD):
            self.mm(pt[:, 0:n], self.mean_b, sq[:, k, 0:n], start=(k == 0), stop=(k == KD - 1), r=[tag + "sq"], w=[pk])
        self.act(sd[:, 0:n], pt[:, 0:n], AF.Sqrt, r=[pk], w=[tag + "sd"], bias=1e-6)
        self.recip(rstd[:, 0:n], sd[:, 0:n], r=[tag + "sd"], w=[tag + "rstd"])
        for k in range(KD):
            oap, okeys = out_fn(k)
            self.stt(oap, hblk[:, k, 0:n], self.prm[:, wcol0 + k:wcol0 + k + 1], rstd[:, 0:n], ALU.mult, ALU.mult,
                     r=[tag + "h", tag + "rstd"], w=okeys)

    def carve_norm(self, base):
        B = base
        hblk = [self.view(B + i * 4096, 4096, F32, "p (k t) -> p k t", k=KD) for i in range(2)]
        sq = self.view(B + 8192, 2048, BF16, "p (k t) -> p k t", k=KD)
        rstd = [self.view(B + 10240 + i * 512, 512) for i in range(2)]
        sd = [self.view(B + 11264 + i * 512, 512) for i in range(2)]
        return hblk, sq, rstd, sd

    def phase_norm1(self, l, s):
        B = self.B0
        hblk, sq, rstd, sd = self.carve_norm(B)
        hn32 = self.view(B + 12288, 4096, F32, "p (k t) -> p k t", k=KD)
        hlo = [self.view(B + 16384 + i * 2048, 2048, BF16, "p (k t) -> p k t", k=KD) for i in range(2)]
        wbaf = self.view(B + 20480, 128, F32, "p (k c) -> p k c", k=KD)
        whi = self.view(B + 20608, 64, BF16, "p (k c) -> p k c", k=KD)
        wlo = self.view(B + 20672, 64, BF16, "p (k c) -> p k c", k=KD)
        win = self.x_win[l].rearrange("(k p) c -> p k c", p=128)
        self.dma("sp", wbaf, win[:, :, 2048:2064], w=["wbaf"])
        self.copy("dve", whi, wbaf, r=["wbaf"], w=["whi"])
        self.tt("dve", wlo, wbaf, whi, ALU.subtract, r=["wbaf", "whi"], w=["wlo"])
        pkb, ptb = ("ps", 7), self.psb[7]
        for bi, (t0, n) in enumerate(tok_blocks(0, ST)):
            g0 = s * ST + t0
            i = bi % 2
            hk = "hb%d" % i
            self.dma("sp", hblk[i][:, :, 0:n], self.hres[:, :, g0:g0 + n],
                     r=[("h", t) for t in range(g0 // 128, (g0 + n) // 128)], w=[hk + "h"])
            self.norm_block(hblk[i], n, sq, rstd[i], sd[i], O_NW1 + l * 8,
                            lambda k, n=n: (hn32[:, k, 0:n], ["hn32"]), hk)
            self.copy("pool", self.hn[:, :, t0:t0 + n], hn32[:, :, 0:n], r=["hn32"], w=["hn"])
            self.tt("pool", hlo[i][:, :, 0:n], hn32[:, :, 0:n], self.hn[:, :, t0:t0 + n], ALU.subtract,
                    r=["hn32", "hn"], w=["hlo%d" % i])
            for tt_ in range(n // 128):
                n_ = t0 // 128 + tt_
                cs = slice(tt_ * 128, (tt_ + 1) * 128)
                gs = slice(t0 + tt_ * 128, t0 + (tt_ + 1) * 128)
                cnt = 0
                for (lh, lk, rw, rk) in ((self.hn, "hn", whi, "whi"), (self.hn, "hn", wlo, "wlo"), (hlo[i], "hlo%d" % i, whi, "whi")):
                    for k in range(KD):
                        lhsT = lh[:, k, gs] if lh is self.hn else lh[:, k, cs]
                        self.mm(ptb[:, n_ * 16:(n_ + 1) * 16], lhsT, rw[:, k, :], start=(cnt == 0), stop=(cnt == 23),
                                r=[lk, rk], w=[pkb])
                        cnt += 1
        ptv = ptb[:, 0:NCH * 16].rearrange("p (n c) -> p n c", n=NCH)
        self.copy("dve", self.ba_tok[:], ptv, r=[pkb], w=["ba"])

    def phase_proj(self, l, s):
        B = self.B0
        wst = [self.view(B + 17408 + i * 2048, 2048, BF16, "p (k c) -> p k c", k=KD) for i in range(4)]
        raw = [self.view(B + i * 2184, 2178) for i in range(3)]
        cv = [self.view(B + 6552 + i * 2176, 2176) for i in range(2)]
        sqb = self.view(B + 10904, 1088, BF16)
        rn = self.view(B + 11992, 2176)
        sdn = [self.view(B + 14168 + i * 512, 512) for i in range(2)]
        wba = self.view(B + 15192, 64, BF16, "p (k c) -> p k c", k=KD)
        win = self.x_win[l].rearrange("(k p) c -> p k c", p=128)
        blocks = tok_blocks(0, ST)
        for i in range(3):
            self.memset("pool", raw[i][:, 0:1], 0.0, w=["raw%d" % i])
            self.memset("pool", raw[i][:, 2177:2178], 0.0, w=["raw%d" % i])
        fam_cols = [0, 512, 1024, 1536, 2064, 2576, 3088]
        def load_group(gi):
            c0 = fam_cols[gi]
            self.dma("pool", wst[gi % 4][:], win[:, :, c0:c0 + 512], w=["wst%d" % (gi % 4)])
        for gi in range(3):
            load_group(gi)

        def project(gi, c, rawi):
            wt = wst[gi % 4]
            for bi, (t0, n) in enumerate(blocks):
                pk, pt = self.ps()
                for k in range(KD):
                    self.mm(pt[:, 0:n], wt[:, k, c * 128:(c + 1) * 128], self.hn[:, k, t0:t0 + n],
                            start=(k == 0), stop=(k == KD - 1), r=["wst%d" % (gi % 4), "hn"], w=[pk])
                eng = "act" if (bi % 2 == 0) else "dve"
                self.copy(eng, raw[rawi][:, 1 + t0:1 + t0 + n], pt[:, 0:n], r=[pk], w=["raw%d" % rawi])

        def conv3(dst, src, wcol0, stride, srck, dstk):
            p = self.prm
            self.ts("dve", dst[:, 0:ST], src[:, 0:ST], p[:, wcol0:wcol0 + 1], None, ALU.mult, r=[srck], w=[dstk])
            for j in (1, 2):
                cc = wcol0 + j * stride
                self.stt(dst[:, 0:ST], src[:, j:j + ST], p[:, cc:cc + 1], dst[:, 0:ST], ALU.mult, ALU.add,
                         r=[srck, dstk], w=[dstk])

        ci = 0
        for gi in range(4):
            if gi + 3 < 7:
                load_group(gi + 3)
            for c in range(4):
                ri = ci % 2
                cvi = ci % 2
                ci += 1
                project(gi, c, ri)
                rk = "raw%d" % ri
                ck = "cv%d" % cvi
                dst = self.qkv[:, gi * 4 + c, :] if gi < 3 else None
                if gi < 3:
                    conv3(cv[cvi], raw[ri], O_CQ + l * 36 + gi * 4 + c, 12, rk, ck)
                    if gi == 2:
                        self.act(dst, cv[cvi][:, 0:ST], AF.Silu, r=[ck], w=["qkv"])
                        self.memset("pool", dst[:, 0:NPAD], 0.0, w=["qkv"])
                    else:
                        self.act(cv[cvi][:, 0:ST], cv[cvi][:, 0:ST], AF.Silu, r=[ck], w=[ck])
                        self.act(sqb[:, 0:ST], cv[cvi][:, 0:ST], AF.Square, r=[ck], w=["sqb"])
                        for bi, (t0, n) in enumerate(blocks):
                            pk, pt = self.ps()
                            self.mm(pt[:, 0:n], self.ones_b, sqb[:, t0:t0 + n], r=["sqb"], w=[pk])
                            self.act(sdn[bi % 2][:, 0:n], pt[:, 0:n], AF.Sqrt, r=[pk], w=["sdn%d" % (bi % 2)], bias=1e-6)
                            self.recip(rn[:, t0:t0 + n], sdn[bi % 2][:, 0:n], r=["sdn%d" % (bi % 2)], w=["rn"])
                        qs = (128.0 ** -0.5) if gi == 0 else 1.0
                        self.stt(dst, cv[cvi][:, 0:ST], qs, rn[:, 0:ST], ALU.mult, ALU.mult, r=[ck, "rn"], w=["qkv"])
                        self.memset("pool", dst[:, 0:NPAD], 0.0, w=["qkv"])
                else:
                    self.act(cv[cvi][:, 0:ST], raw[ri][:, 1:1 + ST], AF.Silu, r=[rk], w=[ck])
                    self.ts("dve", sqb[:, 0:ST], cv[cvi][:, 0:ST], self.prm[:, O_DN + l:O_DN + l + 1], None, ALU.mult,
                            r=[ck, "sqb"], w=["sqb"])
                    self.dma("sp", self.gsc[:, c, :], sqb[:, 0:ST], r=["sqb"], w=[("gsc", c)])
        for c in range(4):
            for j, gi in enumerate((4, 5, 6)):
                project(gi, c, j)
            self.tt("dve", raw[1][:, 1:1 + ST], raw[1][:, 1:1 + ST], raw[2][:, 1:1 + ST], ALU.mult,
                    r=["raw1", "raw2"], w=["raw1"])
            conv3(cv[0], raw[1], O_CS + l * 12 + c, 4, "raw1", "cv0")
            self.tt("dve", sqb[:, 0:ST], cv[0][:, 0:ST], raw[0][:, 1:1 + ST], ALU.mult, r=["cv0", "raw0", "sqb"], w=["sqb"])
            self.dma("sp", self.gsc[:, 4 + c, :], sqb[:, 0:ST], r=["sqb"], w=[("gsc", 4 + c)])
        self.act(self.gb_tok[:, :, 0:8], self.ba_tok[:, :, 0:8], AF.Sigmoid, r=["ba"], w=["gb"])
        al = self.prm[:, O_AL + l * 8:O_AL + l * 8 + 8]
        dtb = self.prm[:, O_DT + l * 8:O_DT + l * 8 + 8]
        self.act(self.nega[:], al, AF.Exp, r=["prm"], w=["nega"])
        self.ts("dve", self.nega[:], self.nega[:], -1.0, None, ALU.mult, r=["nega"], w=["nega"])
        x_, ax, ex, rl = (self.sp_t[:, i, :, :] for i in range(4))
        self.tt("dve", x_, self.ba_tok[:, :, 8:16], dtb.unsqueeze(1).broadcast_to([128, NCH, 8]), ALU.add,
                r=["ba", "prm"], w=["spx"])
        self.act(ax, x_, AF.Abs, r=["spx"], w=["spa"])
        self.act(ex, ax, AF.Exp, r=["spa"], w=["spe"], scale=-1.0)
        self.act(ex, ex, AF.Ln, r=["spe"], w=["spe"], bias=1.0)
        self.ts("dve", rl, x_, 0.0, None, ALU.max, r=["spx"], w=["spr"])
        self.tt("dve", rl, rl, ex, ALU.add, r=["spr", "spe"], w=["spr"])
        self.tt("dve", self.gb_tok[:, :, 8:16], rl, self.nega[:].unsqueeze(1).broadcast_to([128, NCH, 8]), ALU.mult,
                r=["spr", "nega", "gb"], w=["gb"])
        self.memset("pool", self.gb_tok[0:NPAD, 0, :], 0.0, w=["gb"])

    def phase_delta(self, l, s):
        B = self.B0
        qkv = self.qkv
        cst = self.cst
        R3 = "p (h c) -> p h c"

        def f32t(off):
            return self.view(off, 512, F32, R3, h=H)

        def bf16t(off):
            return self.view(off, 256, BF16, R3, h=H)

        bufs = []
        for d in range(2):
            o = B + d * 12640
            nb = {}
            for i, nm in enumerate(["X1", "X2", "E1", "E2", "E3", "u0", "u1", "S32", "R"]):
                nb[nm] = f32t(o + i * 512)
            nb["bE2"] = nb["X2"]
            nb["RB"] = nb["R"]
            o2 = o + 9 * 512
            bfn = ["M", "N", "kbg", "vb", "eGrow", "Wa", "Wb", "Pa", "Pb", "PTa", "PTb", "vnew", "Sbf",
                   "Md", "Nd", "M64", "N64", "M128", "N128", "Ta", "Tb", "Xa", "Xb",
                   "attnT0", "attnT1", "kdec0", "kdec1", "qdT0", "qdT1", "wT0", "wT1"]
            for i, nm in enumerate(bfn):
                nb[nm] = bf16t(o2 + i * 256)
            o3 = o2 + len(bfn) * 256
            nb["eAll0"] = self.view(o3 + 32, 8)
            nb["eAll1"] = self.view(o3 + 40, 8)
            nb["G3a"] = self.view(o3, 8)
            nb["Gd"] = self.view(o3 + 8, 4)
            nb["nGc"] = self.view(o3 + 12, 4)
            nb["eGd"] = self.view(o3 + 24, 4)
            nb["bG"] = self.view(o3 + 28, 4)
            bufs.append(nb)

        def Kbase(d, nm):
            nm = {"bE2": "X2", "RB": "R"}.get(nm, nm)
            return "d%d_%s" % (d, nm)
        K = Kbase

        def bc_h(ap2d):
            return ap2d.unsqueeze(1).broadcast_to([128, H, 128])

        def bc_c(ap2d):
            return ap2d.unsqueeze(2).broadcast_to([128, H, 128])

        def precompute(d, n, par):
            b = dict(bufs[d])
            for nm_ in ("u", "attnT", "kdec", "qdT", "wT", "eAll"):
                b[nm_] = b[nm_ + str(par)]
            K0 = Kbase

            def K(d_, nm_):
                if nm_ in ("u", "attnT", "kdec", "qdT", "wT", "eAll"):
                    nm_ = nm_ + str(par)
                return K0(d_, nm_)
            c0, c1 = n * 128, (n + 1) * 128
            tri = cst[:, 2 + d, :]
            neg1 = cst[:, 4 + d, :]
            neg2 = cst[:, 5 - d, :]
            g4 = self.gb_tok[:, n, 8 + 4 * d:12 + 4 * d]
            b4 = self.gb_tok[:, n, 4 * d:4 * d + 4]
            qT = qkv[:, 0:4, c0:c1]
            kT = qkv[:, 4:8, c0:c1]
            vT = qkv[:, 8:12, c0:c1]
            pkA, ptA = self.ps()
            self.mm(ptA[:, 0:4], tri, g4, r=["gb", "cst"], w=[pkA])
            self.mm(ptA[:, 4:8], self.ones_f, g4, r=["gb", "cst"], w=[pkA])
            self.copy("dve", b["G3a"], ptA[:, 0:8], r=[pkA], w=[K(d, "G3a")])
            yield
            self.tt("dve", b["Gd"], b["G3a"][:, 4:8], b["G3a"][:, 0:4], ALU.subtract, r=[K(d, "G3a")], w=[K(d, "Gd")])
            self.ts("dve", b["nGc"], b["G3a"][:, 0:4], -1.0, None, ALU.mult, r=[K(d, "G3a")], w=[K(d, "nGc")])
            self.act(b["eAll"], b["G3a"], AF.Exp, r=[K(d, "G3a")], w=[K(d, "eAll")])
            self.act(b["eGd"], b["Gd"], AF.Exp, r=[K(d, "Gd")], w=[K(d, "eGd")])
            self.tt("dve", b["bG"], b4, b["eAll"][:, 0:4], ALU.mult, r=["gb", K(d, "eAll")], w=[K(d, "bG")])
            yield
            self.tt("pool", b["R"], bc_h(tri), bc_c(g4), ALU.mult, r=["gb", "cst"], w=[K(d, "R")])
            pkG, ptG = self.ps()
            pkB, ptB = self.ps()
            self.mm(ptG[:], self.ones_f, b["R"].rearrange("p h c -> p (h c)"), r=[K(d, "R"), "cst"], w=[pkG])
            self.tt("pool", b["RB"], bc_h(self.ident_f), bc_c(b4), ALU.mult, r=["gb", "cst"], w=[K(d, "RB")])
            self.mm(ptB[:], self.ones_f, b["RB"].rearrange("p h c -> p (h c)"), r=[K(d, "RB"), "cst"], w=[pkB])
            ptG3 = ptG[:].rearrange(R3, h=H)
            ptB3 = ptB[:].rearrange(R3, h=H)
            yield
            self.tt("dve", b["X1"], bc_h(neg1), ptG3, ALU.subtract, r=[pkG, "cst"], w=[K(d, "X1")])
            self.tt("dve", b["X2"], ptG3, bc_h(neg2), ALU.add, r=[pkG, "cst"], w=[K(d, "X2")])
            self.act(b["eGrow"], ptG3, AF.Exp, r=[pkG], w=[K(d, "eGrow")])
            yield
            for h in range(H):
                self.act(b["E1"][:, h, :], b["X1"][:, h, :], AF.Exp, r=[K(d, "X1"), K(d, "G3a")], w=[K(d, "E1")],
                         bias=b["G3a"][:, h:h + 1])
                self.act(b["E2"][:, h, :], b["X2"][:, h, :], AF.Exp, r=[K(d, "X2"), K(d, "nGc")], w=[K(d, "E2")],
                         bias=b["nGc"][:, h:h + 1])
            yield
            self.tt("pool", b["E3"], b["E2"], bc_h(self.ident_f), ALU.add, r=[K(d, "E2"), "cst"], w=[K(d, "E3")])
            self.tt("dve", b["bE2"], ptB3, b["E2"], ALU.mult, r=[pkB, K(d, "E2")], w=[K(d, "bE2")])
            self.tt("pool", b["qdT"], qT, b["eGrow"], ALU.mult, r=["qkv", K(d, "eGrow")], w=[K(d, "qdT")])
            yield
            pkK, ptK = self.ps()
            pkQ, ptQ = self.ps()
            for h in range(H):
                self.mm(ptK[:, h * 128:(h + 1) * 128], kT[:, h, :], kT[:, h, :], r=["qkv"], w=[pkK])
            for h in range(H):
                self.mm(ptQ[:, h * 128:(h + 1) * 128], kT[:, h, :], qT[:, h, :], r=["qkv"], w=[pkQ])
            ptK3 = ptK[:].rearrange(R3, h=H)
            ptQ3 = ptQ[:].rearrange(R3, h=H)
            yield
            for h in range(H):
                self.stt(b["M"][:, h, :], ptK3[:, h, :], b4[:, h:h + 1], b["E1"][:, h, :], ALU.mult, ALU.mult,
                         r=[pkK, "gb", K(d, "E1")], w=[K(d, "M")])
            self.tt("dve", b["N"], ptK3, b["bE2"], ALU.mult, r=[pkK, K(d, "bE2")], w=[K(d, "N")])
            self.tt("dve", b["attnT"], ptQ3, b["E3"], ALU.mult, r=[pkQ, K(d, "E3")], w=[K(d, "attnT")])
            yield
            pkT, ptT = self.ps()
            pkV, ptV = self.ps()
            for h in range(H):
                self.mm(ptT[:, h * 128:(h + 1) * 128], kT[:, h, :], self.ident_b, r=["qkv", "cbf"], w=[pkT])
            for h in range(H):
                self.mm(ptV[:, h * 128:(h + 1) * 128], vT[:, h, :], self.ident_b, r=["qkv", "cbf"], w=[pkV])
            ptT3 = ptT[:].rearrange(R3, h=H)
            ptV3 = ptV[:].rearrange(R3, h=H)
            yield
            self.tt("dve", b["kbg"], ptT3, bc_c(b["bG"]), ALU.mult, r=[pkT, K(d, "bG")], w=[K(d, "kbg")])
            self.tt("dve", b["kdec"], ptT3, bc_c(b["eGd"]), ALU.mult, r=[pkT, K(d, "eGd")], w=[K(d, "kdec")])
            self.tt("dve", b["vb"], ptV3, bc_c(b4), ALU.mult, r=[pkV, "gb"], w=[K(d, "vb")])
            yield
            bd32, off64, off128 = (self.cbf[:, i, :] for i in (3, 4, 5))
            for (dst, src, msk) in (("Md", "M", bd32), ("Nd", "N", bd32), ("M64", "M", off64), ("N64", "N", off64),
                                    ("M128", "M", off128), ("N128", "N", off128)):
                self.tt("pool", b[dst], b[src], bc_h(msk), ALU.mult, r=[K(d, src), "cbf"], w=[K(d, dst)])
            self.tt("pool", b["Wa"], bc_h(self.ident_b), b["Nd"], ALU.subtract, r=["cbf", K(d, "Nd")], w=[K(d, "Wa")])
            self.tt("pool", b["Ta"], bc_h(self.ident_b), b["Md"], ALU.subtract, r=["cbf", K(d, "Md")], w=[K(d, "Ta")])
            yield

            def mm4(pt, lh, rh):
                for h in range(H):
                    self.mm(pt[0][:, h * 128:(h + 1) * 128], b[lh][:, h, :], b[rh][:, h, :], r=[K(d, lh), K(d, rh)], w=[pt[1]])

            def psum3():
                pk_, pt_ = self.ps()
                return (pt_, pk_)
            Pc, PTc, Wc, Tc = "Md", "Nd", "Wa", "Ta"
            nxt = {"Md": "Pa", "Pa": "Pb", "Pb": "Pa", "Nd": "PTa", "PTa": "PTb", "PTb": "PTa",
                   "Wa": "Wb", "Wb": "Wa", "Ta": "Tb", "Tb": "Ta"}
            for lev in range(4):
                Pn, PTn, Wn, Tn = nxt[Pc], nxt[PTc], nxt[Wc], nxt[Tc]
                p1 = psum3(); p2 = psum3()
                mm4(p1, PTc, Pc)
                mm4(p2, Pc, PTc)
                self.copy("act", b[Pn], p1[0][:].rearrange(R3, h=H), r=[p1[1]], w=[K(d, Pn)])
                self.copy("act", b[PTn], p2[0][:].rearrange(R3, h=H), r=[p2[1]], w=[K(d, PTn)])
                yield
                p3 = psum3(); p4 = psum3()
                mm4(p3, Pn, Wc)
                mm4(p4, PTn, Tc)
                self.tt("dve", b[Wn], p3[0][:].rearrange(R3, h=H), b[Wc], ALU.add, r=[p3[1], K(d, Wc)], w=[K(d, Wn)])
                self.tt("dve", b[Tn], p4[0][:].rearrange(R3, h=H), b[Tc], ALU.add, r=[p4[1], K(d, Tc)], w=[K(d, Tn)])
                Pc, PTc, Wc, Tc = Pn, PTn, Wn, Tn
                yield
            Wn, Tn = nxt[Wc], nxt[Tc]
            p1 = psum3(); p2 = psum3()
            mm4(p1, "N64", Tc)
            mm4(p2, "M64", Wc)
            self.copy("act", b["Xa"], p1[0][:].rearrange(R3, h=H), r=[p1[1]], w=[K(d, "Xa")])
            self.copy("act", b["Xb"], p2[0][:].rearrange(R3, h=H), r=[p2[1]], w=[K(d, "Xb")])
            yield
            p3 = psum3(); p4 = psum3()
            mm4(p3, Wc, "Xa")
            mm4(p4, Tc, "Xb")
            self.tt("dve", b[Tn], b[Tc], p3[0][:].rearrange(R3, h=H), ALU.subtract, r=[p3[1], K(d, Tc)], w=[K(d, Tn)])
            self.tt("dve", b[Wn], b[Wc], p4[0][:].rearrange(R3, h=H), ALU.subtract, r=[p4[1], K(d, Wc)], w=[K(d, Wn)])
            Wc, Tc = Wn, Tn
            yield
            Wn = nxt[Wc]
            p1 = psum3()
            mm4(p1, "M128", Wc)
            self.copy("act", b["Xb"], p1[0][:].rearrange(R3, h=H), r=[p1[1]], w=[K(d, "Xb")])
            yield
            p2 = psum3()
            mm4(p2, Tc, "Xb")
            self.tt("dve", b[Wn], b[Wc], p2[0][:].rearrange(R3, h=H), ALU.subtract, r=[p2[1], K(d, Wc)], w=[K(d, Wn)])
            Wc = Wn
            yield
            pkU, ptU = self.ps()
            pkW, ptW = self.ps()
            for h in range(H):
                self.mm(ptU[:, h * 128:(h + 1) * 128], b[Wc][:, h, :], b["vb"][:, h, :],
                        r=[K(d, Wc), K(d, "vb")], w=[pkU])
            for h in range(H):
                self.mm(ptW[:, h * 128:(h + 1) * 128], b["kbg"][:, h, :], b[Wc][:, h, :],
                        r=[K(d, Wc), K(d, "kbg")], w=[pkW])
            self.copy("act", b["u"], ptU[:].rearrange(R3, h=H), r=[pkU], w=[K(d, "u")])
            self.copy("dve", b["wT"], ptW[:].rearrange(R3, h=H), r=[pkW], w=[K(d, "wT")])
            yield

        o_written = {}

        def scan(d, n, par):
            b = dict(bufs[d])
            for nm_ in ("u", "attnT", "kdec", "qdT", "wT", "eAll"):
                b[nm_] = b[nm_ + str(par)]
            K0 = Kbase

            def K(d_, nm_):
                if nm_ in ("u", "attnT", "kdec", "qdT", "wT", "eAll"):
                    nm_ = nm_ + str(par)
                return K0(d_, nm_)
            pk1, pt1 = self.ps()
            for h in range(H):
                self.mm(pt1[:, h * 128:(h + 1) * 128], b["wT"][:, h, :], b["Sbf"][:, h, :],
                        r=[K(d, "wT"), K(d, "Sbf")], w=[pk1])
            self.tt("dve", b["vnew"], b["u"], pt1[:].rearrange(R3, h=H), ALU.subtract,
                    r=[K(d, "u"), pk1], w=[K(d, "vnew")])
            yield
            pkO, ptO = self.ps()
            pkS, ptS = self.ps()
            for h in range(H):
                self.mm(ptO[:, h * 128:(h + 1) * 128], b["qdT"][:, h, :], b["Sbf"][:, h, :], start=True, stop=False,
                        r=[K(d, "qdT"), K(d, "Sbf")], w=[pkO])
                self.mm(ptO[:, h * 128:(h + 1) * 128], b["attnT"][:, h, :], b["vnew"][:, h, :], start=False, stop=True,
                        r=[K(d, "attnT"), K(d, "vnew")], w=[pkO])
            for h in range(H):
                self.mm(ptS[:, h * 128:(h + 1) * 128], b["kdec"][:, h, :], b["vnew"][:, h, :],
                        r=[K(d, "kdec"), K(d, "vnew")], w=[pkS])
            ok = ("oacc", n)
            if n not in o_written:
                o_written[n] = True
                self.copy("act", self.o_acc[:, n, :], ptO[:], r=[pkO], w=[ok])
            else:
                self.tt("dve", self.o_acc[:, n, :], ptO[:], self.o_acc[:, n, :], ALU.add, r=[pkO, ok], w=[ok])
            for h in range(H):
                self.stt(b["S32"][:, h, :], b["S32"][:, h, :], b["eAll"][:, 4 + h:5 + h],
                         ptS[:, h * 128:(h + 1) * 128], ALU.mult, ALU.add,
                         r=[K(d, "S32"), K(d, "eAll"), pkS], w=[K(d, "S32")])
            self.copy("act", b["Sbf"], b["S32"], r=[K(d, "S32")], w=[K(d, "Sbf")])
            yield

        def run_rr(gens):
            gens = list(gens)
            while gens:
                for g in list(gens):
                    try:
                        next(g)
                    except StopIteration:
                        gens.remove(g)

        for d in range(2):
            self.memset("pool", bufs[d]["S32"], 0.0, w=[K(d, "S32")])
            self.memset("pool", bufs[d]["Sbf"], 0.0, w=[K(d, "Sbf")])
        run_rr([precompute(0, 0, 0), precompute(1, NCH - 1, 0)])
        for step in range(NCH):
            nf_, nb_ = step, NCH - 1 - step
            gens = [scan(0, nf_, step % 2), scan(1, nb_, step % 2)]
            if step + 1 < NCH:
                gens += [precompute(0, nf_ + 1, (step + 1) % 2), precompute(1, nb_ - 1, (step + 1) % 2)]
            run_rr(gens)
        self.S.barrier()
        if self.dbg and l == 0 and s == 0:
            self.dma("sp", self.d_o, self.o_acc, w=["d_o"])
            self.S.barrier()
        gate = self.view(B, 4352, BF16, "p (c t) -> p c t", c=4)
        ysc_dst = self.qkv[:, 4:8, :]
        self.dma("sp", gate, self.gsc[:, 0:4, :], r=[("gsc", c) for c in range(4)], w=["gate"])
        self.dma("sp", ysc_dst, self.gsc[:, 4:8, :], r=[("gsc", c) for c in range(4, 8)], w=["ysc"])
        o2 = B + 4352
        sqo = [self.view(o2 + i * 512, 512, F32, R3, h=H) for i in range(2)]
        onb = [self.view(o2 + 1024 + i * 256, 256, BF16, R3, h=H) for i in range(2)]
        ss = [self.view(o2 + 1536 + i * 4, 4) for i in range(2)]
        sdv = [self.view(o2 + 1544 + i * 4, 4) for i in range(2)]
        rnv = [self.view(o2 + 1552 + i * 4, 4) for i in range(2)]
        for n in range(NCH):
            i = n % 2
            c0, c1 = n * 128, (n + 1) * 128
            oa = self.o_acc[:, n, :].rearrange(R3, h=H)
            self.tt("pool", sqo[i], oa, oa, ALU.mult, w=["sqo%d" % i])
            self.reduce(ss[i], sqo[i], r=["sqo%d" % i], w=["ss%d" % i])
            self.act(sdv[i], ss[i], AF.Sqrt, r=["ss%d" % i], w=["sdv%d" % i], bias=1e-6, scale=1.0 / 128)
            self.recip(rnv[i], sdv[i], r=["sdv%d" % i], w=["rnv%d" % i])
            self.tt("dve", onb[i], oa, rnv[i].unsqueeze(2).broadcast_to([128, H, 128]), ALU.mult,
                    r=["rnv%d" % i], w=["onb%d" % i])
            pk, pt = self.ps()
            for h in range(H):
                self.mm(pt[:, h * 128:(h + 1) * 128], onb[i][:, h, :], self.ident_b, r=["onb%d" % i], w=[pk])
            self.tt("dve", self.qkv[:, 0:4, c0:c1], pt[:].rearrange(R3, h=H), gate[:, :, c0:c1], ALU.mult,
                    r=[pk, "gate"], w=["mixo"])

    def phase_outproj(self, l, s):
        B = self.B0
        hblk, sq, rstd, sd = self.carve_norm(B)
        wo = self.view(B + 12288, 4096, BF16, "p (k c) -> p k c", k=KD)
        hn2b = [self.view(B + 16384 + i * 2048, 2048, BF16, "p (k t) -> p k t", k=KD) for i in range(2)]
        self.dma("pool", wo[:], self.x_wout[l].rearrange("(k p) c -> p k c", p=128), w=["wo"])
        for bi, (t0, n) in enumerate(tok_blocks(0, ST)):
            g0 = s * ST + t0
            i = bi % 2
            hk = "hb%d" % i
            hkeys = [("h", t) for t in range(g0 // 128, (g0 + n) // 128)]
            self.dma("sp", hblk[i][:, :, 0:n], self.hres[:, :, g0:g0 + n], r=hkeys, w=[hk + "h"])
            for j in range(KD):
                pk, pt = self.ps()
                for k in range(KD):
                    self.mm(pt[:, 0:n], wo[:, k, j * 128:(j + 1) * 128], self.qkv[:, k, t0:t0 + n],
                            start=(k == 0), stop=(k == KD - 1), r=["wo", "mixo", "ysc"], w=[pk])
                self.tt("dve", hblk[i][:, j, 0:n], pt[:, 0:n], hblk[i][:, j, 0:n], ALU.add, r=[pk, hk + "h"], w=[hk + "h"])
            self.dma("sp", self.hres[:, :, g0:g0 + n], hblk[i][:, :, 0:n], r=[hk + "h"], w=hkeys)
            self.norm_block(hblk[i], n, sq, rstd[i], sd[i], O_NW2 + l * 8,
                            lambda k, i=i, n=n: (hn2b[i][:, k, 0:n], ["hn2b%d" % i]), hk)
            self.dma("sp", self.hn2[:, :, g0:g0 + n], hn2b[i][:, :, 0:n], r=["hn2b%d" % i],
                     w=[("hn2", t) for t in range(g0 // 128, (g0 + n) // 128)])

    def phase_ffn(self, l):
        wgu = [self.view(i * 6144, 6144, BF16, "p (k g c) -> p k g c", k=KD, g=2) for i in range(2)]
        wd = [self.view(12288 + i * 3072, 3072, BF16, "p (f c) -> p f c", f=6) for i in range(2)]
        hblk = [self.view(18432 + i * 4096, 4096, F32, "p (k t) -> p k t", k=KD) for i in range(2)]
        hnb = [self.view(26624 + i * 2048, 2048, BF16, "p (k t) -> p k t", k=KD) for i in range(2)]
        sbuf_ = [self.view(30720 + i * 512, 512) for i in range(2)]
        actb = [self.view(31744 + i * 1536, 1536, BF16, "p (f t) -> p f t", f=6) for i in range(2)]
        wgu_d = self.x_wgu[l].rearrange("(k p) c -> p k c", p=128)

        def load_w(gi):
            f0, f1 = FGROUPS[gi]
            nf = f1 - f0
            i = gi % 2
            self.dma("pool", wgu[i][:, :, 0, 0:nf * 128], wgu_d[:, :, f0 * 128:f1 * 128], w=["wgu%d" % i])
            self.dma("pool", wgu[i][:, :, 1, 0:nf * 128], wgu_d[:, :, DFF + f0 * 128:DFF + f1 * 128], w=["wgu%d" % i])
            self.dma("pool", wd[i][:, 0:nf, :], self.x_wd[l][f0 * 128:f1 * 128, :].rearrange("(f p) c -> p f c", p=128),
                     w=["wd%d" % i])

        load_w(0)
        blocks = tok_blocks(0, TOK)
        it = 0
        for gi, (f0, f1) in enumerate(FGROUPS):
            nf = f1 - f0
            wi = gi % 2
            if gi + 1 < len(FGROUPS):
                load_w(gi + 1)
            for bi, (t0, n) in enumerate(blocks):
                i = it % 2
                it += 1
                tkeys = [("h", t) for t in range(t0 // 128, (t0 + n) // 128)]
                nkeys = [("hn2", t) for t in range(t0 // 128, (t0 + n) // 128)]
                self.dma("sp", hnb[i][:, :, 0:n], self.hn2[:, :, t0:t0 + n], r=nkeys, w=["hnb%d" % i])
                self.dma("sp", hblk[i][:, :, 0:n], self.hres[:, :, t0:t0 + n], r=tkeys, w=["fh%d" % i])
                for f in range(nf):
                    pkg, ptg = self.ps()
                    pku, ptu = self.ps()
                    for k in range(KD):
                        self.mm(ptg[:, 0:n], wgu[wi][:, k, 0, f * 128:(f + 1) * 128], hnb[i][:, k, 0:n],
                                start=(k == 0), stop=(k == KD - 1), r=["wgu%d" % wi, "hnb%d" % i], w=[pkg])
                    for k in range(KD):
                        self.mm(ptu[:, 0:n], wgu[wi][:, k, 1, f * 128:(f + 1) * 128], hnb[i][:, k, 0:n],
                                start=(k == 0), stop=(k == KD - 1), r=["wgu%d" % wi, "hnb%d" % i], w=[pku])
                    si = f % 2
                    self.act(sbuf_[si][:, 0:n], ptg[:, 0:n], AF.Silu, r=[pkg], w=["sg%d" % si])
                    self.tt("dve", actb[i][:, f, 0:n], ptu[:, 0:n], sbuf_[si][:, 0:n], ALU.mult,
                            r=[pku, "sg%d" % si], w=["actb%d" % i])
                for j in range(KD):
                    pk, pt = self.ps()
                    for f in range(nf):
                        self.mm(pt[:, 0:n], wd[wi][:, f, j * 128:(j + 1) * 128], actb[i][:, f, 0:n],
                                start=(f == 0), stop=(f == nf - 1), r=["wd%d" % wi, "actb%d" % i], w=[pk])
                    self.tt("dve", hblk[i][:, j, 0:n], pt[:, 0:n], hblk[i][:, j, 0:n], ALU.add,
                            r=[pk, "fh%d" % i], w=["fh%d" % i])
                self.dma("sp", self.hres[:, :, t0:t0 + n], hblk[i][:, :, 0:n], r=["fh%d" % i], w=tkeys)

    def phase_final(self):
        hblk, sq, rstd, sd = self.carve_norm(self.B0)
        ob = [self.view(self.B0 + 12288 + i * 4096, 4096, F32, "p (k t) -> p k t", k=KD) for i in range(2)]
        for bi, (t0, n) in enumerate(tok_blocks(0, TOK)):
            i = bi % 2
            hk = "hb%d" % i
            self.dma("sp", hblk[i][:, :, 0:n], self.hres[:, :, t0:t0 + n],
                     r=[("h", t) for t in range(t0 // 128, (t0 + n) // 128)], w=[hk + "h"])
            self.norm_block(hblk[i], n, sq, rstd[i], sd[i], O_FNW,
                            lambda k, i=i, n=n: (ob[i][:, k, 0:n], ["ob%d" % i]), hk)
            self.dma("sp", self.y_out[:, :, t0:t0 + n], ob[i][:, :, 0:n], r=["ob%d" % i], w=[("out", bi)])


def make_consts():
    r = np.arange(128)[:, None]
    c = np.arange(128)[None, :]
    cst = np.zeros((128, 9, 128), np.float32)
    cst[:, 6, :] = (r // 32 == c // 32)
    cst[:, 7, :] = (r // 64 == c // 64) & (r // 32 != c // 32)
    cst[:, 8, :] = (r // 64 != c // 64)
    cst[:, 0, :] = (r == c)
    cst[:, 1, :] = 1.0
    cst[:, 2, :] = (r <= c)
    cst[:, 3, :] = (r >= c)
    cst[:, 4, :] = np.where(c < r, 0.0, NEGBIG)
    cst[:, 5, :] = np.where(c > r, 0.0, NEGBIG)
    return cst


def make_prm(norm1_w, norm2_w, final_norm_w, conv_qkv_w, sc_conv_w, dn_norm_w, a_log, dt_bias):
    prm = np.zeros((128, NPRM), np.float32)
    for l in range(L_FULL):
        prm[:, O_NW1 + l * 8:O_NW1 + l * 8 + 8] = norm1_w[l].reshape(8, 128).T
        prm[:, O_NW2 + l * 8:O_NW2 + l * 8 + 8] = norm2_w[l].reshape(8, 128).T
        prm[:, O_CQ + l * 36:O_CQ + l * 36 + 36] = conv_qkv_w[l].reshape(3, 12, 128).transpose(2, 0, 1).reshape(128, 36)
        prm[:, O_CS + l * 12:O_CS + l * 12 + 12] = sc_conv_w[l].reshape(3, 4, 128).transpose(2, 0, 1).reshape(128, 12)
        prm[:, O_DN + l] = dn_norm_w[l]
        prm[:, O_AL + l * 8:O_AL + l * 8 + 8] = a_log[l].reshape(1, 8)
        prm[:, O_DT + l * 8:O_DT + l * 8 + 8] = dt_bias[l].reshape(1, 8)
    prm[:, O_FNW:O_FNW + 8] = final_norm_w.reshape(8, 128).T
    return prm


def make_h0(x, meta_tokens, core):
    arr = np.zeros((TOK, D), np.float32)
    for s in range(SPC):
        arr[s * ST + NPAD:s * ST + NPAD + NMETA] = meta_tokens
        arr[s * ST + NPAD + NMETA:(s + 1) * ST] = x[core * SPC + s]
    return np.ascontiguousarray(arr.reshape(TOK, KD, 128).transpose(2, 1, 0))


_NC_CACHE = {}


def run(inputs, n_layers=L_FULL, ncores=NCORES, dbg=False, stop=999):
    f32 = lambda a: np.ascontiguousarray(np.asarray(a, dtype=np.float32))
    x = f32(inputs["x"])
    key = (n_layers, dbg, stop)
    if key not in _NC_CACHE:
        _NC_CACHE[key] = Builder(n_layers, dbg, stop).build()
    nc = _NC_CACHE[key]
    prm = make_prm(f32(inputs["norm1_w"]), f32(inputs["norm2_w"]), f32(inputs["final_norm_w"]),
                   f32(inputs["conv_qkv_w"]), f32(inputs["sc_conv_w"]), f32(inputs["dn_norm_w"]),
                   f32(inputs["a_log"]), f32(inputs["dt_bias"]))
    cst = make_consts()
    w_in, w_out, w_gu, w_d = f32(inputs["w_in"]), f32(inputs["w_out"]), f32(inputs["w_gate_up"]), f32(inputs["w_down"])
    meta = f32(inputs["meta_tokens"])
    in_maps = []
    for c in range(ncores):
        in_maps.append(dict(h0=make_h0(x, meta, c), prm=prm, cst=cst, w_in=w_in, w_out=w_out, w_gu=w_gu, w_down=w_d))
    res = run_bass_kernel_spmd(nc, in_maps, core_ids=list(range(ncores)))
    return res


def kernel(**inputs):
    res = run(inputs)
    out = np.zeros((BATCH, SEQ, D), np.float32)
    for c in range(NCORES):
        oT = np.asarray(res.results[c]["outT"], dtype=np.float32)
        arr = oT.transpose(2, 1, 0).reshape(TOK, D)
        for s in range(SPC):
            out[c * SPC + s] = arr[s * ST + NPAD + NMETA:(s + 1) * ST]
    return out
```
